# Optimizing a Trainium2 kernel written in Bass

```python
import math
import jax, jax.numpy as jnp
from jax import lax
import numpy as np

D_MODEL = 1024
BATCH = 2
SEQ = 8192
DEPTH = 2

CHUNK = 64
Q_BLOCK = 128
EPS = 1e-6

MLSTM_HEADS = 4
MLSTM_HD = D_MODEL // MLSTM_HEADS
MLSTM_W = MLSTM_HEADS * MLSTM_HD
CONV_K = 4

DIFF_HEADS = 8
DIFF_HD = D_MODEL // (2 * DIFF_HEADS)
DIFF_W = DIFF_HEADS * 2 * DIFF_HD

RET_HEADS = 4
RET_HD = D_MODEL // RET_HEADS
RET_W = RET_HEADS * RET_HD

FFN_HIDDEN = -(-8 * D_MODEL // (3 * 256)) * 256

IN_SIZES = (MLSTM_W, MLSTM_W, MLSTM_W, MLSTM_W, MLSTM_HEADS, MLSTM_HEADS,
            DIFF_W, DIFF_W, DIFF_W,
            RET_W, RET_W, RET_W, RET_W,
            3 * D_MODEL)
N_IN = sum(IN_SIZES)

kernel_name = "hybrid_mlstm_diffattn_retention_block"


def rmsnorm(x, g):
    x32 = x.astype(jnp.float32)
    y = x32 * lax.rsqrt(jnp.mean(x32 * x32, axis=-1, keepdims=True) + EPS)
    return (y * g.astype(jnp.float32)).astype(x.dtype)


def head_rms(x):
    x32 = x.astype(jnp.float32)
    return x32 * lax.rsqrt(jnp.mean(x32 * x32, axis=-1, keepdims=True) + EPS)


def causal_dwconv(x, w):
    K = w.shape[0]
    S = x.shape[1]
    xp = jnp.pad(x, ((0, 0), (K - 1, 0), (0, 0)))
    out = xp[:, 0:S] * w[0]
    for j in range(1, K):
        out = out + xp[:, j:j + S] * w[j]
    return out


def to_chunks(a):
    B, S = a.shape[0], a.shape[1]
    return jnp.moveaxis(a.reshape(B, S // CHUNK, CHUNK, *a.shape[2:]), 1, 0)


def from_chunks(a):
    a = jnp.moveaxis(a, 0, 1)
    return a.reshape(a.shape[0], a.shape[1] * a.shape[2], *a.shape[3:])


def mlstm(q, k, v, i_pre, f_pre):
    B, S, H, dh = q.shape
    f32 = jnp.float32
    q = q.astype(f32)
    k = k.astype(f32) * (dh ** -0.5)
    v = v.astype(f32)
    logf = jax.nn.log_sigmoid(f_pre.astype(f32))
    ig = i_pre.astype(f32)
    causal = jnp.tril(jnp.ones((CHUNK, CHUNK), dtype=bool))

    def step(carry, inp):
        C, n, m = carry
        qj, kj, vj, lf, igj = inp
        b = jnp.cumsum(lf, axis=1).transpose(0, 2, 1)
        igT = igj.transpose(0, 2, 1)
        g = b[..., -1]
        d_intra = b[..., :, None] - b[..., None, :] + igT[..., None, :]
        d_intra = jnp.where(causal, d_intra, -jnp.inf)
        d_inter = b + m[..., None]
        m_row = jnp.maximum(jnp.max(d_intra, axis=-1), d_inter)
        w_intra = jnp.exp(d_intra - m_row[..., None])
        w_inter = jnp.exp(d_inter - m_row)
        s = jnp.einsum('blhd,bshd->bhls', qj, kj) * w_intra
        w_int_t = w_inter.transpose(0, 2, 1)[..., None]
        num = (jnp.einsum('bhls,bshe->blhe', s, vj)
               + jnp.einsum('blhd,bhde->blhe', qj, C) * w_int_t)
        den = jnp.sum(s, axis=-1) + jnp.einsum('blhd,bhd->bhl', qj, n) * w_inter
        den = jnp.maximum(jnp.abs(den), jnp.exp(-m_row))
        h = num / den.transpose(0, 2, 1)[..., None]
        a = g[..., None] - b + igT
        m_new = jnp.maximum(g + m, jnp.max(a, axis=-1))
        w_old = jnp.exp(g + m - m_new)
        w_k = jnp.exp(a - m_new[..., None])
        C_new = w_old[..., None, None] * C + jnp.einsum('bhs,bshd,bshe->bhde', w_k, kj, vj)
        n_new = w_old[..., None] * n + jnp.einsum('bhs,bshd->bhd', w_k, kj)
        return (C_new, n_new, m_new), h

    init = (jnp.zeros((B, H, dh, dh), f32), jnp.zeros((B, H, dh), f32), jnp.zeros((B, H), f32))
    xs = (to_chunks(q), to_chunks(k), to_chunks(v), to_chunks(logf), to_chunks(ig))
    _, hs = lax.scan(step, init, xs)
    return from_chunks(hs)


def retention(q, k, v):
    B, S, H, dh = q.shape
    f32 = jnp.float32
    q = q.astype(f32)
    k = k.astype(f32) * (dh ** -0.5)
    v = v.astype(f32)
    log_gamma = jnp.log(1.0 - 2.0 ** (-5.0 - jnp.arange(H, dtype=f32)))
    pos = jnp.arange(CHUNK, dtype=f32)
    rel = pos[:, None] - pos[None, :]
    dmask = jnp.where(rel[None] >= 0, jnp.exp(log_gamma[:, None, None] * jnp.maximum(rel, 0.0)[None]), 0.0)
    xi = jnp.exp(log_gamma[None, :] * (pos[:, None] + 1.0))
    zeta = jnp.exp(log_gamma[:, None] * (CHUNK - 1.0 - pos[None, :]))
    chunk_decay = jnp.exp(log_gamma * CHUNK)

    def step(R, inp):
        qj, kj, vj = inp
        s = jnp.einsum('blhd,bshd->bhls', qj, kj) * dmask[None]
        o = (jnp.einsum('bhls,bshe->blhe', s, vj)
             + jnp.einsum('blhd,bhde->blhe', qj, R) * xi[None, :, :, None])
        R_new = chunk_decay[None, :, None, None] * R + jnp.einsum('hs,bshd,bshe->bhde', zeta, kj, vj)
        return R_new, o

    R0 = jnp.zeros((B, H, dh, dh), f32)
    _, os_ = lax.scan(step, R0, (to_chunks(q), to_chunks(k), to_chunks(v)))
    return from_chunks(os_)


def diff_attention(q, k, v, lam):
    B, S, H, _, dh = q.shape
    f32 = jnp.float32
    q = q.astype(f32) * (dh ** -0.5)
    k = k.astype(f32)
    v = v.astype(f32)
    slopes = 2.0 ** (-8.0 * (jnp.arange(H, dtype=f32) + 1.0) / H)
    outs = []
    for blk in range(S // Q_BLOCK):
        q0 = blk * Q_BLOCK
        kend = q0 + Q_BLOCK
        qb = q[:, q0:kend]
        kb = k[:, :kend]
        vb = v[:, :kend]
        s = jnp.einsum('bqhcd,bkhcd->bhcqk', qb, kb)
        tq = jnp.arange(q0, kend)
        tk = jnp.arange(kend)
        dist = jnp.abs(tq[:, None] - tk[None, :]).astype(f32)
        allowed = (tk // CHUNK)[None, :] <= (tq // CHUNK)[:, None]
        bias = -slopes[:, None, None] * dist[None]
        s = jnp.where(allowed, s + bias[None, :, None], -jnp.inf)
        p = jax.nn.softmax(s, axis=-1)
        a = p[:, :, 0] - lam * p[:, :, 1]
        outs.append(jnp.einsum('bhqk,bkhe->bqhe', a, vb))
    return jnp.concatenate(outs, axis=1)


def split_in(proj):
    idx = np.cumsum(np.array(IN_SIZES))[:-1].tolist()
    return jnp.split(proj, idx, axis=-1)


def setup_inputs(seed: int = 0) -> dict:
    key = jax.random.key(seed)
    ks = jax.random.split(key, 16)
    f32 = jnp.float32

    def nrm(kk, shape, scale):
        return jax.random.normal(kk, shape, f32) * scale

    x = nrm(ks[0], (BATCH, SEQ, D_MODEL), 1.0)
    norm_gains = 1.0 + nrm(ks[1], (DEPTH, 4, D_MODEL), 0.05)
    w_in = nrm(ks[2], (DEPTH, D_MODEL, N_IN), D_MODEL ** -0.5)
    mlstm_conv = nrm(ks[3], (DEPTH, CONV_K, 2 * MLSTM_W), CONV_K ** -0.5)
    f_bias = jnp.broadcast_to(jnp.linspace(3.0, 6.0, MLSTM_HEADS, dtype=f32), (DEPTH, MLSTM_HEADS))
    i_bias = jnp.zeros((DEPTH, MLSTM_HEADS), f32)
    mlstm_gate_bias = jnp.stack([i_bias, f_bias], axis=1) + nrm(ks[4], (DEPTH, 2, MLSTM_HEADS), 0.1)
    diff_lambda = nrm(ks[5], (DEPTH, 4, DIFF_HD), 0.1)
    diff_subln = 1.0 + nrm(ks[6], (DEPTH, 2 * DIFF_HD), 0.05)
    w_mlstm_out = nrm(ks[7], (DEPTH, MLSTM_W, D_MODEL), MLSTM_W ** -0.5)
    w_diff_out = nrm(ks[8], (DEPTH, DIFF_W, D_MODEL), DIFF_W ** -0.5)
    w_ret_out = nrm(ks[9], (DEPTH, RET_W, D_MODEL), RET_W ** -0.5)
    w_out = nrm(ks[10], (DEPTH, D_MODEL, D_MODEL), D_MODEL ** -0.5)
    w_ffn_in = nrm(ks[11], (DEPTH, D_MODEL, 2 * FFN_HIDDEN), D_MODEL ** -0.5)
    w_ffn_out = nrm(ks[12], (DEPTH, FFN_HIDDEN, D_MODEL), FFN_HIDDEN ** -0.5)
    return {"x": x, "norm_gains": norm_gains, "w_in": w_in, "mlstm_conv": mlstm_conv,
            "mlstm_gate_bias": mlstm_gate_bias, "diff_lambda": diff_lambda,
            "diff_subln": diff_subln, "w_mlstm_out": w_mlstm_out, "w_diff_out": w_diff_out,
            "w_ret_out": w_ret_out, "w_out": w_out, "w_ffn_in": w_ffn_in,
            "w_ffn_out": w_ffn_out}


def reference(x, norm_gains, w_in, mlstm_conv, mlstm_gate_bias, diff_lambda, diff_subln,
              w_mlstm_out, w_diff_out, w_ret_out, w_out, w_ffn_in, w_ffn_out):
    B, S, _ = x.shape
    f32 = jnp.float32
    for l in range(DEPTH):
        g = norm_gains[l]
        h = rmsnorm(x, g[0])
        proj = h @ w_in[l]
        (mq, mk, mv, mo, mi, mf, dq, dk, dv, rq, rk, rv, rg, gates) = split_in(proj)

        qk = jax.nn.silu(causal_dwconv(jnp.concatenate([mq, mk], axis=-1), mlstm_conv[l]))
        mq_c, mk_c = jnp.split(qk, 2, axis=-1)
        hm = mlstm(mq_c.reshape(B, S, MLSTM_HEADS, MLSTM_HD),
                   mk_c.reshape(B, S, MLSTM_HEADS, MLSTM_HD),
                   mv.reshape(B, S, MLSTM_HEADS, MLSTM_HD),
                   mi + mlstm_gate_bias[l, 0],
                   mf + mlstm_gate_bias[l, 1])
        hm = jax.nn.sigmoid(mo.astype(f32)) * head_rms(hm).reshape(B, S, MLSTM_W)

        lam_init = 0.8 - 0.6 * math.exp(-0.3 * l)
        lmb = diff_lambda[l].astype(f32)
        lam = jnp.exp(jnp.sum(lmb[0] * lmb[1])) - jnp.exp(jnp.sum(lmb[2] * lmb[3])) + lam_init
        hd = diff_attention(dq.reshape(B, S, DIFF_HEADS, 2, DIFF_HD),
                            dk.reshape(B, S, DIFF_HEADS, 2, DIFF_HD),
                            dv.reshape(B, S, DIFF_HEADS, 2 * DIFF_HD), lam)
        hd = (rmsnorm(hd, diff_subln[l]) * (1.0 - lam_init)).reshape(B, S, DIFF_W)

        hr = retention(rq.reshape(B, S, RET_HEADS, RET_HD),
                       rk.reshape(B, S, RET_HEADS, RET_HD),
                       rv.reshape(B, S, RET_HEADS, RET_HD))
        hr = jax.nn.silu(rg.astype(f32)) * head_rms(hr).reshape(B, S, RET_W)

        gm, gd, gr = jnp.split(jax.nn.sigmoid(gates), 3, axis=-1)
        y = (gm * (hm.astype(x.dtype) @ w_mlstm_out[l])
             + gd * (hd.astype(x.dtype) @ w_diff_out[l])
             + gr * (hr.astype(x.dtype) @ w_ret_out[l]))
        y = y @ w_out[l]
        x = x + rmsnorm(y, g[1])

        h2 = rmsnorm(x, g[2])
        a, u = jnp.split(h2 @ w_ffn_in[l], 2, axis=-1)
        f = (jax.nn.silu(a) * u) @ w_ffn_out[l]
        x = x + rmsnorm(f, g[3])
    return x
```

```python
import math
import types
from contextlib import ExitStack

import numpy as np
import ml_dtypes
import concourse.bass as bass
import concourse.mybir as mybir
from concourse.bass_utils import run_bass_kernel_spmd

F32 = mybir.dt.float32
BF16 = mybir.dt.bfloat16
AF = mybir.ActivationFunctionType
ALU = mybir.AluOpType
AX = mybir.AxisListType
NPBF = ml_dtypes.bfloat16

D = 1024
B = 2
S = 8192
DEPTH = 2
NCORE = 8
FH = 2816
EPS = 1e-6
TOK = S * B // NCORE
N_IN = 14344


class Buf:
    __slots__ = ("w", "r")

    def __init__(self):
        self.w = {}
        self.r = {}


def _freeze(fn):
    if fn.__closure__ is None:
        return fn
    cells = tuple(types.CellType(c.cell_contents) for c in fn.__closure__)
    return types.FunctionType(fn.__code__, fn.__globals__, fn.__name__, fn.__defaults__, cells)


class Prog:
    ENGS = ("pe", "act", "dve", "pool", "sp")
    NDMA = 6

    def __init__(self, nc, ctx):
        self.nc = nc
        self.ctx = ctx
        self.sems = {}
        for e in self.ENGS:
            self.sems[e] = ctx.enter_context(nc.semaphore("s_" + e))
        self.cnt = {e: 0 for e in self.ENGS}
        self.dma_val = {}
        self.dma_rr = {}
        for q in ("sp", "act", "pool"):
            for i in range(self.NDMA):
                k = "d_%s%d" % (q, i)
                self.sems[k] = ctx.enter_context(nc.semaphore(k))
                self.dma_val[k] = 0
            self.dma_rr[q] = 0
        self.seen = {e: {} for e in self.ENGS}
        self.stream = {e: [] for e in self.ENGS}
        self.uid = 0
        self.pf = []
        for i in range(6):
            t = ctx.enter_context(nc.psum_tensor("pf%d" % i, [128, 512], F32))
            self.pf.append((t, Buf()))
        self.pb = []
        for i in range(2):
            t = ctx.enter_context(nc.psum_tensor("pb%d" % i, [128, 1024], BF16))
            self.pb.append((t, Buf()))
        self.pf_i = 0
        self.pb_i = 0
        self.scr = []
        for i in range(12):
            t = ctx.enter_context(nc.sbuf_tensor("scr%d" % i, [128, 8], F32))
            self.scr.append((t, Buf()))
        self.scr_i = 0

    def sb(self, shape, dt, name=None):
        self.uid += 1
        t = self.ctx.enter_context(self.nc.sbuf_tensor("%s_%d" % (name or "t", self.uid), shape, dt))
        return t, Buf()

    def bank(self):
        r = self.pf[self.pf_i]
        self.pf_i = (self.pf_i + 1) % len(self.pf)
        return r

    def bankb(self):
        r = self.pb[self.pb_i]
        self.pb_i = (self.pb_i + 1) % len(self.pb)
        return r

    def scratch(self):
        r = self.scr[self.scr_i]
        self.scr_i = (self.scr_i + 1) % len(self.scr)
        return r

    def _wait(self, eng, deps):
        for k, v in deps.items():
            if eng == "pe" and k == "pe":
                continue
            if self.seen[eng].get(k, 0) < v:
                self.seen[eng][k] = v
                sem = self.sems[k]
                self.stream[eng].append(lambda E, sem=sem, v=v: E.wait_ge(sem, v))

    @staticmethod
    def _deps(reads, writes):
        deps = {}
        for b in reads:
            for k, v in b.w.items():
                if deps.get(k, 0) < v:
                    deps[k] = v
        for b in writes:
            for k, v in b.w.items():
                if deps.get(k, 0) < v:
                    deps[k] = v
            for k, v in b.r.items():
                if deps.get(k, 0) < v:
                    deps[k] = v
        return deps

    def op(self, eng, fn, reads=(), writes=()):
        self._wait(eng, self._deps(reads, writes))
        self.cnt[eng] += 1
        sem = self.sems[eng]
        fn = _freeze(fn)
        self.stream[eng].append(lambda E, fn=fn, sem=sem: fn(E).then_inc(sem, 1))
        v = self.cnt[eng]
        for b in writes:
            b.w[eng] = v
        for b in reads:
            if b.r.get(eng, 0) < v:
                b.r[eng] = v

    def dma(self, q, out, in_, reads=(), writes=()):
        i = self.dma_rr[q]
        self.dma_rr[q] = (i + 1) % self.NDMA
        k = "d_%s%d" % (q, i)
        deps = self._deps(reads, writes)
        if self.dma_val[k] > 0:
            deps[k] = max(deps.get(k, 0), self.dma_val[k])
        self._wait(q, deps)
        self.dma_val[k] += 16
        v = self.dma_val[k]
        sem = self.sems[k]
        self.stream[q].append(
            lambda E, out=out, in_=in_, sem=sem: E.dma_start(out=out, in_=in_).then_inc(sem, 16))
        for b in writes:
            b.w[k] = v
        for b in reads:
            if b.r.get(k, 0) < v:
                b.r[k] = v

    def barrier(self):
        allv = {e: self.cnt[e] for e in self.ENGS if self.cnt[e] > 0}
        for k, v in self.dma_val.items():
            if v > 0:
                allv[k] = v
        for e in self.ENGS:
            self._wait(e, dict(allv))

    def finish(self):
        allv = {e: self.cnt[e] for e in self.ENGS if self.cnt[e] > 0}
        for k, v in self.dma_val.items():
            if v > 0:
                allv[k] = v
        self._wait("sp", allv)

    def emit(self):
        st = self.stream
        with self.nc.Block() as block:
            @block.tensor
            def _(E):
                for f in st["pe"]:
                    f(E)

            @block.scalar
            def _(E):
                for f in st["act"]:
                    f(E)

            @block.vector
            def _(E):
                for f in st["dve"]:
                    f(E)

            @block.gpsimd
            def _(E):
                for f in st["pool"]:
                    f(E)

            @block.sync
            def _(E):
                for f in st["sp"]:
                    f(E)


def rstd_from(P, parts, n, eps=EPS):
    t, bt = P.scratch()
    np_ = parts[0][0].shape[0]
    o = t[0:np_, 0:1]
    if len(parts) == 1:
        (a, ba), = parts
        P.op("dve", lambda E: E.tensor_scalar(out=o, in0=a, scalar1=1.0 / n, scalar2=eps,
                                              op0=ALU.mult, op1=ALU.add), reads=[ba], writes=[bt])
    else:
        (a, ba), (c, bc) = parts
        P.op("dve", lambda E: E.tensor_tensor(out=o, in0=a, in1=c, op=ALU.add), reads=[ba, bc], writes=[bt])
        P.op("dve", lambda E: E.tensor_scalar(out=o, in0=o, scalar1=1.0 / n, scalar2=eps,
                                              op0=ALU.mult, op1=ALU.add), reads=[bt], writes=[bt])
    P.op("act", lambda E: E.sqrt(out=o, in_=o), reads=[bt], writes=[bt])
    P.op("dve", lambda E: E.reciprocal(out=o, in_=o), reads=[bt], writes=[bt])
    return o, bt


def sumsq(P, src, bsrc, junk, bjunk):
    t, bt = P.scratch()
    np_ = src.shape[0]
    o = t[0:np_, 0:1]
    P.op("act", lambda E: E.activation(out=junk, in_=src, func=AF.Square, accum_out=o),
         reads=[bsrc], writes=[bjunk, bt])
    return o, bt


def norm_to_T(P, xs, bxs, G, bG, ident, bI, hb, bhb, junk, bjunk, dstT, bdst, col0):
    ss = sumsq(P, xs, bxs, junk, bjunk)
    r, br = rstd_from(P, [ss], D)
    P.op("dve", lambda E: E.scalar_tensor_tensor(out=hb, in0=xs, scalar=r, in1=G, op0=ALU.mult, op1=ALU.mult),
         reads=[bxs, br, bG], writes=[bhb])
    pt, bpt = P.bankb()
    ptv = pt[:].rearrange("p (k n) -> p k n", k=8)
    for k in range(8):
        P.op("pe", lambda E, k=k: E.transpose(out=ptv[:, k, :], in_=hb[:, k * 128:(k + 1) * 128], identity=ident),
             reads=[bhb, bI], writes=[bpt])
    P.op("act", lambda E: E.copy(out=dstT[:, :, col0:col0 + 128], in_=ptv), reads=[bpt], writes=[bdst])


def build_k1():
    nc = bass.Bass("TRN2", target_bir_lowering=False)
    x = nc.dram_tensor("x", [TOK, D], F32, kind="ExternalInput").ap()
    g = nc.dram_tensor("g", [1, D], F32, kind="ExternalInput").ap()
    ident_d = nc.dram_tensor("ident", [128, 128], BF16, kind="ExternalInput").ap()
    hT = nc.dram_tensor("hT", [D, TOK], BF16, kind="ExternalOutput").ap()
    with ExitStack() as ctx:
        P = Prog(nc, ctx)
        G, bG = P.sb([128, D], F32)
        ident, bI = P.sb([128, 128], BF16)
        P.dma("sp", G[:], g.broadcast_to([128, D]), writes=[bG])
        P.dma("sp", ident[:], ident_d, writes=[bI])
        xs = [P.sb([128, D], F32) for _ in range(3)]
        hbs = [P.sb([128, D], BF16) for _ in range(2)]
        junk, bjunk = P.sb([128, D], F32)
        stg = [P.sb([128, 8, 512], BF16) for _ in range(2)]
        hTv = hT.rearrange("(k p) n -> p k n", p=128)
        for t in range(TOK // 512):
            st, bst = stg[t % 2]
            for s in range(4):
                i = t * 4 + s
                xt, bx = xs[i % 3]
                P.dma("sp", xt[:], x[i * 128:(i + 1) * 128, :], writes=[bx])
                hb, bhb = hbs[i % 2]
                norm_to_T(P, xt[:], bx, G[:], bG, ident[:], bI, hb[:], bhb, junk[:], bjunk, st, bst, s * 128)
            P.dma("sp", hTv[:, :, t * 512:(t + 1) * 512], st[:], reads=[bst], writes=[Buf()])
        P.finish()
        P.emit()
    return nc


def build_k3():
    nc = bass.Bass("TRN2", target_bir_lowering=False)
    x = nc.dram_tensor("x", [TOK, D], F32, kind="ExternalInput").ap()
    hT = nc.dram_tensor("hT", [D, TOK], BF16, kind="ExternalInput").ap()
    bT = nc.dram_tensor("bT", [3, D, TOK], BF16, kind="ExternalInput").ap()
    wa_d = nc.dram_tensor("wa", [8, 128, 6 * 8 * 128], F32, kind="ExternalInput").ap()
    wo_d = nc.dram_tensor("wo", [128, 8 * D], F32, kind="ExternalInput").ap()
    wf_d = nc.dram_tensor("wf", [FH // 128, 128, 2 * 8 * 128], F32, kind="ExternalInput").ap()
    wfo_d = nc.dram_tensor("wfo", [2, 128, (FH // 128) * 512], F32, kind="ExternalInput").ap()
    g_d = nc.dram_tensor("g", [4, D], F32, kind="ExternalInput").ap()
    ident_d = nc.dram_tensor("ident", [128, 128], BF16, kind="ExternalInput").ap()
    xo = nc.dram_tensor("xo", [TOK, D], F32, kind="ExternalOutput").ap()
    hTn = nc.dram_tensor("hTn", [D, TOK], BF16, kind="ExternalOutput").ap()
    NT = TOK // 512
    NF = FH // 128
    with ExitStack() as ctx:
        P = Prog(nc, ctx)
        ident, bI = P.sb([128, 128], BF16)
        P.dma("sp", ident[:], ident_d, writes=[bI])
        Gs = []
        for i in range(4):
            G, bG = P.sb([128, D], F32)
            P.dma("sp", G[:], g_d[i:i + 1, :].broadcast_to([128, D]), writes=[bG])
            Gs.append((G, bG))
        junk, bjunk = P.sb([128, D], F32)
        wo, bwo = P.sb([128, 8, D], BF16)
        P.dma("pool", wo[:].rearrange("p k n -> p (k n)"), wo_d, writes=[bwo])
        was = [P.sb([128, 6, 8, 128], BF16) for _ in range(2)]
        wfs = [P.sb([128, 2, 8, 128], BF16) for _ in range(2)]
        wfo, bwfo = P.sb([128, NF, 512], BF16)
        ht, bht = P.sb([128, 8, 512], BF16)
        arena, bbt = P.sb([128, 3 * 8 * 512], BF16)
        bt_ = arena[:].rearrange("p (c k n) -> p c k n", c=3, k=8)
        yT, byT = P.sb([128, 8, 512], BF16)
        h2T, bh2T = P.sb([128, 8, 512], BF16)
        sT = arena[:, 0:NF * 512].rearrange("p (k n) -> p k n", k=NF)
        bsT = bbt
        f0, _ = P.sb([128, 4, 512], F32)
        bf0 = [Buf() for _ in range(4)]
        x1s, _ = P.sb([128, 4, D], F32)
        bx1 = [Buf() for _ in range(4)]
        xts = [P.sb([128, D], F32) for _ in range(2)]
        hbs = [P.sb([128, D], BF16) for _ in range(2)]
        gts = [P.sb([128, 512], F32) for _ in range(3)]
        tmps = [P.sb([128, 512], F32) for _ in range(3)]
        sas = [P.sb([128, 512], F32) for _ in range(2)]
        ti = 0
        wi = 0
        fi = 0
        hTv = hT.rearrange("(k p) n -> p k n", p=128)

        def out_norm_add(halves, G, bG, src, bsrc, dst, bdst):
            nonlocal ti
            sss = [sumsq(P, pz, bpz, junk[:, 0:512], bjunk) for pz, bpz in halves]
            r, br = rstd_from(P, sss, D)
            for hf, (pz, bpz) in enumerate(halves):
                tm, btm = tmps[ti % 3]
                ti += 1
                P.op("dve", lambda E, tm=tm, pz=pz, hf=hf: E.scalar_tensor_tensor(
                    out=tm[:], in0=pz, scalar=r, in1=G[:, hf * 512:(hf + 1) * 512],
                    op0=ALU.mult, op1=ALU.mult), reads=[bpz, br, bG], writes=[btm])
                P.op("pool", lambda E, tm=tm, hf=hf: E.tensor_tensor(
                    out=dst[:, hf * 512:(hf + 1) * 512], in0=src[:, hf * 512:(hf + 1) * 512],
                    in1=tm[:], op=ALU.add), reads=[btm, bsrc], writes=[bdst])

        for t in range(NT):
            P.dma("sp", ht[:], hTv[:, :, t * 512:(t + 1) * 512], writes=[bht])
            for c in range(3):
                P.dma("sp", bt_[:, c, :, :],
                      bT[c].rearrange("(k p) n -> p k n", p=128)[:, :, t * 512:(t + 1) * 512], writes=[bbt])
            for oc in range(8):
                wa, bwa = was[wi % 2]
                wi += 1
                P.dma("pool", wa[:].rearrange("p a k n -> p (a k n)"), wa_d[oc], writes=[bwa])
                terms = []
                for c in range(3):
                    pg, bpg = P.bank()
                    for k in range(8):
                        P.op("pe", lambda E, k=k, c=c, pg=pg, wa=wa: E.matmul(
                            pg[:], lhsT=wa[:, c, k, :], rhs=ht[:, k, :],
                            start=(k == 0), stop=(k == 7)), reads=[bwa, bht], writes=[bpg])
                    gt, bgt = gts[c]
                    P.op("act", lambda E, gt=gt, pg=pg: E.activation(out=gt[:], in_=pg[:], func=AF.Sigmoid),
                         reads=[bpg], writes=[bgt])
                    pp, bpp = P.bank()
                    for k in range(8):
                        P.op("pe", lambda E, k=k, c=c, pp=pp, wa=wa: E.matmul(
                            pp[:], lhsT=wa[:, 3 + c, k, :], rhs=bt_[:, c, k, :],
                            start=(k == 0), stop=(k == 7)), reads=[bwa, bbt], writes=[bpp])
                    tm, btm = tmps[ti % 3]
                    ti += 1
                    P.op("dve", lambda E, tm=tm, gt=gt, pp=pp: E.tensor_tensor(
                        out=tm[:], in0=pp[:], in1=gt[:], op=ALU.mult), reads=[bpp, bgt], writes=[btm])
                    terms.append((tm, btm))
                (t0, b0), (t1, b1), (t2, b2) = terms
                P.op("pool", lambda E, t0=t0, t1=t1: E.tensor_tensor(out=t0[:], in0=t0[:], in1=t1[:], op=ALU.add),
                     reads=[b0, b1], writes=[b0])
                P.op("pool", lambda E, t0=t0, t2=t2, oc=oc: E.tensor_tensor(
                    out=yT[:, oc, :], in0=t0[:], in1=t2[:], op=ALU.add), reads=[b0, b2], writes=[byT])
            for s in range(4):
                i = t * 4 + s
                xt, bxt = xts[i % 2]
                P.dma("sp", xt[:], x[i * 128:(i + 1) * 128, :], writes=[bxt])
                halves = []
                for hf in range(2):
                    pz, bpz = P.bank()
                    for k in range(8):
                        P.op("pe", lambda E, k=k, hf=hf, pz=pz, s=s: E.matmul(
                            pz[:], lhsT=yT[:, k, s * 128:(s + 1) * 128], rhs=wo[:, k, hf * 512:(hf + 1) * 512],
                            start=(k == 0), stop=(k == 7)), reads=[byT, bwo], writes=[bpz])
                    halves.append((pz[:], bpz))
                out_norm_add(halves, Gs[0][0], Gs[0][1], xt, bxt, x1s[:, s, :], bx1[s])
                hb, bhb = hbs[i % 2]
                norm_to_T(P, x1s[:, s, :], bx1[s], Gs[1][0][:], Gs[1][1], ident[:], bI, hb[:], bhb,
                          junk[:], bjunk, h2T, bh2T, s * 128)
            for fc in range(NF):
                wf, bwf = wfs[fi % 2]
                fi += 1
                P.dma("pool", wf[:].rearrange("p a k n -> p (a k n)"), wf_d[fc], writes=[bwf])
                pa, bpa = P.bank()
                for k in range(8):
                    P.op("pe", lambda E, k=k, pa=pa, wf=wf: E.matmul(
                        pa[:], lhsT=wf[:, 0, k, :], rhs=h2T[:, k, :],
                        start=(k == 0), stop=(k == 7)), reads=[bwf, bh2T], writes=[bpa])
                pu, bpu = P.bank()
                for k in range(8):
                    P.op("pe", lambda E, k=k, pu=pu, wf=wf: E.matmul(
                        pu[:], lhsT=wf[:, 1, k, :], rhs=h2T[:, k, :],
                        start=(k == 0), stop=(k == 7)), reads=[bwf, bh2T], writes=[bpu])
                sa, bsa = sas[fc % 2]
                P.op("act", lambda E, sa=sa, pa=pa: E.activation(out=sa[:], in_=pa[:], func=AF.Silu),
                     reads=[bpa], writes=[bsa])
                P.op("dve", lambda E, sa=sa, pu=pu, fc=fc: E.tensor_tensor(
                    out=sT[:, fc, :], in0=pu[:], in1=sa[:], op=ALU.mult), reads=[bpu, bsa], writes=[bsT])
            srcs = [[None, None] for _ in range(4)]
            for hf in range(2):
                P.dma("pool", wfo[:].rearrange("p k n -> p (k n)"), wfo_d[hf], writes=[bwfo])
                pzs = [P.bank() for _ in range(4)]
                for k in range(NF):
                    for s in range(4):
                        pz, bpz = pzs[s]
                        P.op("pe", lambda E, k=k, pz=pz, s=s: E.matmul(
                            pz[:], lhsT=sT[:, k, s * 128:(s + 1) * 128], rhs=wfo[:, k, :],
                            start=(k == 0), stop=(k == NF - 1)), reads=[bsT, bwfo], writes=[bpz])
                for s in range(4):
                    pz, bpz = pzs[s]
                    if hf == 0:
                        P.op("act", lambda E, pz=pz, s=s: E.copy(out=f0[:, s, :], in_=pz[:]),
                             reads=[bpz], writes=[bf0[s]])
                        srcs[s][0] = (f0[:, s, :], bf0[s])
                    else:
                        srcs[s][1] = (pz[:], bpz)
            for s in range(4):
                i = t * 4 + s
                ot, bot = xts[i % 2]
                out_norm_add(srcs[s], Gs[2][0], Gs[2][1], x1s[:, s, :], bx1[s], ot, bot)
                P.dma("sp", xo[i * 128:(i + 1) * 128, :], ot[:], reads=[bot], writes=[Buf()])
                hb, bhb = hbs[i % 2]
                norm_to_T(P, ot[:], bot, Gs[3][0][:], Gs[3][1], ident[:], bI, hb[:], bhb,
                          junk[:], bjunk, ht, bht, s * 128)
            P.dma("sp", hTn.rearrange("(k p) n -> p k n", p=128)[:, :, t * 512:(t + 1) * 512], ht[:],
                  reads=[bht], writes=[Buf()])
        P.finish()
        P.emit()
    return nc


OFF = dict(mq=0, mk=1024, mv=2048, mo=3072, mi=4096, mf=4100, dq=4104, dk=5128, dv=6152,
           rq=7176, rk=8200, rv=9224, rg=10248, gates=11272)
IDENT = np.eye(128, dtype=np.float32).astype(NPBF)


def kp(w):
    k = w.shape[0] // 128
    return w.reshape(k, 128, w.shape[1]).transpose(1, 0, 2)


def prep_k3_weights(w_in, w_m, w_d, w_r, w_out, w_fi, w_fo):
    wg = w_in[:, OFF["gates"]:OFF["gates"] + 3 * D]
    wa = np.empty((8, 128, 6, 8, 128), np.float32)
    for oc in range(8):
        for c in range(3):
            wa[oc, :, c] = kp(wg[:, c * D + oc * 128:c * D + (oc + 1) * 128])
        for c, wb in enumerate((w_m, w_d, w_r)):
            wa[oc, :, 3 + c] = kp(wb[:, oc * 128:(oc + 1) * 128])
    wo = kp(w_out)
    nf = FH // 128
    wf = np.empty((nf, 128, 2, 8, 128), np.float32)
    for fc in range(nf):
        for a in range(2):
            wf[fc, :, a] = kp(w_fi[:, a * FH + fc * 128:a * FH + (fc + 1) * 128])
    wfo = np.empty((2, 128, nf, 512), np.float32)
    for hf in range(2):
        wfo[hf] = kp(w_fo[:, hf * 512:(hf + 1) * 512])
    return dict(wa=np.ascontiguousarray(wa.reshape(8, 128, -1)),
                wo=np.ascontiguousarray(wo.reshape(128, -1)),
                wf=np.ascontiguousarray(wf.reshape(nf, 128, -1)),
                wfo=np.ascontiguousarray(wfo.reshape(2, 128, -1)))


def proj_fm(P, w, bw, c0, m, ht, bht):
    ps, bps = P.bank()
    for k in range(8):
        P.op("pe", lambda E: E.matmul(ps[0:m, :], lhsT=w[:, k, c0:c0 + m], rhs=ht[:, k, :],
                                      start=(k == 0), stop=(k == 7)), reads=[bw, bht], writes=[bps])
    return ps, bps


def proj_tm(P, w, bw, c0, n, ht, bht, t0, m):
    ps, bps = P.bank()
    for k in range(8):
        P.op("pe", lambda E: E.matmul(ps[0:m, 0:n], lhsT=ht[:, k, t0:t0 + m], rhs=w[:, k, c0:c0 + n],
                                      start=(k == 0), stop=(k == 7)), reads=[bw, bht], writes=[bps])
    return ps, bps


def emit_out_T(P, src, bsrc, npart, nch, ident, bI, dst):
    pt, bpt = P.bankb()
    for dc in range(nch):
        P.op("pe", lambda E: E.transpose(out=pt[:, dc * npart:(dc + 1) * npart],
                                         in_=src[:, dc * 128:(dc + 1) * 128],
                                         identity=ident[0:npart, 0:npart]),
             reads=[bsrc, bI], writes=[bpt])
    for dc in range(nch):
        ap, b = dst(dc)
        P.op("act", lambda E: E.copy(out=ap, in_=pt[:, dc * npart:(dc + 1) * npart]), reads=[bpt], writes=[b])


def ret_setup(nc, P, ident, bI):
    w_d = nc.dram_tensor("w_r", [128, 8 * 1024], F32, kind="ExternalInput").ap()
    cst_d = nc.dram_tensor("cst_r", [128, 1024 + 65], F32, kind="ExternalInput").ap()
    oT = nc.dram_tensor("oT_r", [2, 128, S], BF16, kind="ExternalOutput").ap()
    w, bw = P.sb([128, 8, 1024], BF16)
    P.dma("pool", w[:].rearrange("p k n -> p (k n)"), w_d, writes=[bw])
    cst, bc = P.sb([128, 1024 + 65], F32)
    P.dma("sp", cst[:], cst_d, writes=[bc])
    GQ, GK, MT = cst[:, 0:512], cst[:, 512:1024], cst[0:64, 1024:1088]
    G64 = cst[:, 1088:1089]
    R, bR = P.sb([128, 512], F32)
    Rb, bRb = P.sb([128, 512], BF16)
    P.op("dve", lambda E: E.memset(R[:], 0.0), writes=[bR])
    P.op("dve", lambda E: E.memset(Rb[:], 0.0), writes=[bRb])
    qk, bqk = P.sb([128, 4, 512], BF16)
    vs, bvs = P.sb([64, 8, 256], BF16)
    sg, bsg = P.sb([64, 8, 256], F32)
    kts = [P.sb([64, 256], BF16) for _ in range(2)]
    pts = [P.sb([64, 64], BF16) for _ in range(2)]
    hrs = [P.sb([64, 256], BF16) for _ in range(2)]
    junk, bjunk = P.sb([64, 256], F32)
    osts = [P.sb([128, 2, 512], BF16) for _ in range(2)]

    def step(t, ht, bht):
        for cc in range(4):
            ps, bps = proj_fm(P, w, bw, cc * 128, 128, ht, bht)
            P.op("dve", lambda E: E.tensor_tensor(out=qk[:, cc, :], in0=ps[:], in1=(GQ if cc < 2 else GK), op=ALU.mult),
                 reads=[bps, bc], writes=[bqk])
        for c in range(8):
            ps, bps = proj_tm(P, w, bw, 512, 512, ht, bht, c * 64, 64)
            P.op("act", lambda E: E.copy(out=vs[:, c, :], in_=ps[0:64, 0:256]), reads=[bps], writes=[bvs])
            P.op("act", lambda E: E.activation(out=sg[:, c, :], in_=ps[0:64, 256:512], func=AF.Silu),
                 reads=[bps], writes=[bsg])
        ost, bost = osts[t % 2]
        for c in range(8):
            cs = slice(c * 64, (c + 1) * 64)
            n = t * 8 + c
            pk, bpk = P.bankb()
            for dc in range(2):
                P.op("pe", lambda E: E.transpose(out=pk[0:64, dc * 128:(dc + 1) * 128], in_=qk[:, 2 + dc, cs],
                                                 identity=ident[:]), reads=[bqk, bI], writes=[bpk])
            kt, bkt = kts[n % 2]
            P.op("dve", lambda E: E.tensor_scalar(out=kt[:], in0=pk[0:64, 0:256], scalar1=G64[0:64, :], scalar2=None,
                                                  op0=ALU.mult), reads=[bpk, bc], writes=[bkt])
            pS, bpS = P.bank()
            for dc in range(2):
                P.op("pe", lambda E: E.matmul(pS[0:64, 0:64], lhsT=qk[:, 2 + dc, cs], rhs=qk[:, dc, cs],
                                              start=(dc == 0), stop=(dc == 1)), reads=[bqk], writes=[bpS])
            pt, bpt = pts[n % 2]
            P.op("dve", lambda E: E.tensor_tensor(out=pt[:], in0=pS[0:64, 0:64], in1=MT, op=ALU.mult),
                 reads=[bpS, bc], writes=[bpt])
            po, bpo = P.bank()
            P.op("pe", lambda E: E.matmul(po[0:64, 0:256], lhsT=pt[:], rhs=vs[:, c, :], start=True, stop=False),
                 reads=[bpt, bvs], writes=[bpo])
            for dc in range(2):
                P.op("pe", lambda E: E.matmul(po[0:64, 0:256], lhsT=qk[:, dc, cs], rhs=Rb[:, dc * 256:(dc + 1) * 256],
                                              start=False, stop=(dc == 1)), reads=[bqk, bRb], writes=[bpo])
            pu, bpu = P.bank()
            for dc in range(2):
                P.op("pe", lambda E: E.matmul(pu[:, dc * 256:(dc + 1) * 256], lhsT=kt[:, dc * 128:(dc + 1) * 128],
                                              rhs=vs[:, c, :], start=True, stop=True), reads=[bkt, bvs], writes=[bpu])
            P.op("dve", lambda E: E.scalar_tensor_tensor(out=R[:], in0=R[:], scalar=G64, in1=pu[:],
                                                         op0=ALU.mult, op1=ALU.add), reads=[bR, bpu, bc], writes=[bR])
            P.op("act", lambda E: E.copy(out=Rb[:], in_=R[:]), reads=[bR], writes=[bRb])
            ss = sumsq(P, po[0:64, 0:256], bpo, junk[:], bjunk)
            r, br = rstd_from(P, [ss], 256)
            hr, bhr = hrs[n % 2]
            P.op("dve", lambda E: E.scalar_tensor_tensor(out=hr[:], in0=po[0:64, 0:256], scalar=r, in1=sg[:, c, :],
                                                         op0=ALU.mult, op1=ALU.mult), reads=[bpo, br, bsg], writes=[bhr])
            emit_out_T(P, hr, bhr, 64, 2, ident, bI, lambda dc: (ost[:, dc, c * 64:(c + 1) * 64], bost))
        P.dma("sp", oT.rearrange("a p n -> p a n")[:, :, t * 512:(t + 1) * 512], ost[:], reads=[bost], writes=[Buf()])
    return step


def ret_consts(gamma):
    pos = np.arange(512) % 64
    gq = (gamma ** (pos + 1.0)).astype(np.float32)
    gk = (gamma ** (-(pos + 1.0)) / 16.0).astype(np.float32)
    c = np.zeros((128, 1024 + 65), np.float32)
    c[:, 0:512] = gq[None]
    c[:, 512:1024] = gk[None]
    c[0:64, 1024:1088] = np.triu(np.ones((64, 64), np.float32))
    c[:, 1088] = gamma ** 64
    return c


def mlstm_setup(nc, P, ident, bI):
    w_d = nc.dram_tensor("w_m", [128, 8 * 1026], F32, kind="ExternalInput").ap()
    cst_d = nc.dram_tensor("cst_m", [128, 210], F32, kind="ExternalInput").ap()
    oT = nc.dram_tensor("oT_m", [2, 128, S], BF16, kind="ExternalOutput").ap()
    w, bw = P.sb([128, 8, 1026], BF16)
    P.dma("pool", w[:].rearrange("p k n -> p (k n)"), w_d, writes=[bw])
    cst, bc = P.sb([128, 210], F32)
    P.dma("sp", cst[:], cst_d, writes=[bc])
    MT = cst[0:64, 0:64]
    CW = cst[:, 64:80]
    GB = cst[:, 80:82]
    ONES = cst[0:1, 82:210]
    C, bC = P.sb([128, 2, 257], F32)
    Cb, bCb = P.sb([128, 2, 257], BF16)
    P.op("dve", lambda E: E.memset(C[:], 0.0), writes=[bC])
    mrow, bm = P.sb([1, 1], F32)
    P.op("dve", lambda E: E.memset(mrow[:], 0.0), writes=[bm])
    pre, bpre = P.sb([128, 4, 515], F32)
    P.op("dve", lambda E: E.memset(pre[:], 0.0), writes=[bpre])
    acc, bacc = P.sb([128, 512], F32)
    sil, bsil = P.sb([128, 512], F32)
    qk, bqk = P.sb([128, 4, 512], BF16)
    va, bva = P.sb([64, 8, 257], BF16)
    P.op("dve", lambda E: E.memset(va[:], 1.0), writes=[bva])
    so, bso = P.sb([64, 8, 256], F32)
    rows, brows = P.sb([1, 4, 512], F32)
    sm, bsm = P.sb([1, 64], F32)
    rep, brep = P.sb([128, 16], F32)
    WI, bWI = P.sb([128, 8], F32)
    colA, bcolA = P.sb([64, 8, 8], F32)
    kts = [P.sb([64, 256], BF16) for _ in range(2)]
    pts = [P.sb([64, 64], BF16) for _ in range(2)]
    vus = [P.sb([64, 257], BF16) for _ in range(2)]
    hms = [P.sb([64, 256], BF16) for _ in range(2)]
    junk, bjunk = P.sb([64, 256], F32)
    osts = [P.sb([128, 2, 512], BF16) for _ in range(2)]
    col = lambda i: colA[:, :, i]

    def logsig(dst, src, tmp, b):
        P.op("act", lambda E: E.activation(out=tmp, in_=src, func=AF.Abs), reads=[b], writes=[b])
        P.op("act", lambda E: E.activation(out=tmp, in_=tmp, func=AF.Exp, scale=-1.0), reads=[b], writes=[b])
        P.op("act", lambda E: E.activation(out=tmp, in_=tmp, func=AF.Ln, bias=1.0), reads=[b], writes=[b])
        P.op("dve", lambda E: E.tensor_scalar(out=dst, in0=src, scalar1=0.0, scalar2=None, op0=ALU.min), reads=[b], writes=[b])
        P.op("dve", lambda E: E.tensor_tensor(out=dst, in0=dst, in1=tmp, op=ALU.subtract), reads=[b], writes=[b])

    def step(t, ht, bht):
        P.op("dve", lambda E: E.tensor_copy(out=pre[:, :, 0:3], in_=pre[:, :, 512:515]), reads=[bpre], writes=[bpre])
        for cc in range(4):
            ps, bps = proj_fm(P, w, bw, cc * 128, 128, ht, bht)
            P.op("act", lambda E: E.copy(out=pre[:, cc, 3:515], in_=ps[:]), reads=[bps], writes=[bpre])
        for cc in range(4):
            P.op("dve", lambda E: E.tensor_scalar(out=acc[:], in0=pre[:, cc, 0:512], scalar1=CW[:, cc * 4:cc * 4 + 1],
                                                  scalar2=None, op0=ALU.mult), reads=[bpre, bc], writes=[bacc])
            for j in range(1, 4):
                P.op("dve", lambda E: E.scalar_tensor_tensor(out=acc[:], in0=pre[:, cc, j:j + 512],
                                                             scalar=CW[:, cc * 4 + j:cc * 4 + j + 1], in1=acc[:],
                                                             op0=ALU.mult, op1=ALU.add), reads=[bpre, bc, bacc], writes=[bacc])
            if cc < 2:
                P.op("act", lambda E: E.activation(out=qk[:, cc, :], in_=acc[:], func=AF.Silu), reads=[bacc], writes=[bqk])
            else:
                P.op("act", lambda E: E.activation(out=sil[:], in_=acc[:], func=AF.Silu), reads=[bacc], writes=[bsil])
                P.op("pool", lambda E: E.tensor_scalar(out=qk[:, cc, :], in0=sil[:], scalar1=1.0 / 16, scalar2=None,
                                                       op0=ALU.mult), reads=[bsil], writes=[bqk])
        for c in range(8):
            ps, bps = proj_tm(P, w, bw, 512, 512, ht, bht, c * 64, 64)
            P.op("act", lambda E: E.copy(out=va[:, c, 0:256], in_=ps[0:64, 0:256]), reads=[bps], writes=[bva])
            P.op("act", lambda E: E.activation(out=so[:, c, :], in_=ps[0:64, 256:512], func=AF.Sigmoid),
                 reads=[bps], writes=[bso])
        pg, bpg = P.bank()
        for c in range(8):
            for k in range(8):
                P.op("pe", lambda E: E.matmul(pg[0:64, c * 16:c * 16 + 2], lhsT=ht[:, k, c * 64:(c + 1) * 64],
                                              rhs=w[:, k, 1024:1026], start=(k == 0), stop=(k == 7)),
                     reads=[bw, bht], writes=[bpg])
        pgv = pg[0:64, 0:128].rearrange("p (c a) -> p c a", a=16)
        P.op("dve", lambda E: E.tensor_scalar(out=col(0), in0=pgv[:, :, 0], scalar1=GB[0:64, 0:1], scalar2=None,
                                              op0=ALU.add), reads=[bpg, bc], writes=[bcolA])
        P.op("dve", lambda E: E.tensor_scalar(out=col(1), in0=pgv[:, :, 1], scalar1=GB[0:64, 1:2], scalar2=None,
                                              op0=ALU.add), reads=[bpg, bc], writes=[bcolA])
        logsig(col(3), col(1), col(2), bcolA)
        pb_, bpb = P.bank()
        P.op("pe", lambda E: E.matmul(pb_[0:64, 0:8], lhsT=MT, rhs=col(3), start=True, stop=True),
             reads=[bc, bcolA], writes=[bpb])
        P.op("dve", lambda E: E.tensor_copy(out=col(7), in_=pb_[0:64, 0:8]), reads=[bpb], writes=[bcolA])
        P.op("dve", lambda E: E.tensor_tensor(out=col(4), in0=col(0), in1=col(7), op=ALU.subtract),
             reads=[bcolA], writes=[bcolA])
        for a in range(2):
            pr, bpr = P.bank()
            for k in range(8):
                P.op("pe", lambda E: E.matmul(pr[0:1, :], lhsT=w[:, k, 1024 + a:1025 + a], rhs=ht[:, k, :],
                                              start=(k == 0), stop=(k == 7)), reads=[bw, bht], writes=[bpr])
            P.op("dve", lambda E: E.tensor_scalar(out=rows[:, a, :], in0=pr[0:1, :], scalar1=GB[0:1, a:a + 1],
                                                  scalar2=None, op0=ALU.add), reads=[bpr, bc], writes=[brows])
        logsig(rows[:, 3, :], rows[:, 1, :], rows[:, 2, :], brows)
        P.op("dve", lambda E: E.tensor_reduce(out=sm[:, 0:8], in_=rows[:, 0, :].rearrange("p (c n) -> p c n", n=64),
                                              axis=AX.X, op=ALU.max), reads=[brows], writes=[bsm])
        P.op("dve", lambda E: E.tensor_reduce(out=sm[:, 8:16], in_=rows[:, 3, :].rearrange("p (c n) -> p c n", n=64),
                                              axis=AX.X, op=ALU.add), reads=[brows], writes=[bsm])
        P.op("dve", lambda E: E.tensor_tensor(out=sm[:, 16:24], in0=sm[:, 0:8], in1=sm[:, 8:16], op=ALU.subtract),
             reads=[bsm], writes=[bsm])
        for c in range(8):
            P.op("dve", lambda E: E.tensor_copy(out=sm[:, 32 + c:33 + c], in_=mrow[:]), reads=[bm, bsm], writes=[bsm])
            P.op("dve", lambda E: E.tensor_tensor(out=sm[:, 24 + c:25 + c], in0=mrow[:], in1=sm[:, 16 + c:17 + c],
                                                  op=ALU.max), reads=[bm, bsm], writes=[bsm])
            P.op("dve", lambda E: E.tensor_tensor(out=mrow[:], in0=sm[:, 24 + c:25 + c], in1=sm[:, 8 + c:9 + c],
                                                  op=ALU.add), reads=[bsm], writes=[bm])
        prep_, bprep = P.bank()
        P.op("pe", lambda E: E.matmul(prep_[:, 0:16], lhsT=ONES, rhs=sm[:, 24:40], start=True, stop=True),
             reads=[bc, bsm], writes=[bprep])
        P.op("act", lambda E: E.copy(out=rep[:], in_=prep_[:, 0:16]), reads=[bprep], writes=[brep])
        P.op("dve", lambda E: E.tensor_tensor(out=WI[:], in0=rep[:, 8:16], in1=rep[:, 0:8], op=ALU.subtract),
             reads=[brep], writes=[bWI])
        P.op("act", lambda E: E.activation(out=WI[:], in_=WI[:], func=AF.Exp), reads=[bWI], writes=[bWI])
        P.op("dve", lambda E: E.tensor_tensor(out=col(5), in0=col(4), in1=rep[0:64, 0:8], op=ALU.subtract),
             reads=[bcolA, brep], writes=[bcolA])
        P.op("act", lambda E: E.activation(out=col(5), in_=col(5), func=AF.Exp), reads=[bcolA], writes=[bcolA])
        P.op("dve", lambda E: E.tensor_tensor(out=col(6), in0=col(7), in1=rep[0:64, 0:8], op=ALU.add),
             reads=[brep, bcolA], writes=[bcolA])
        P.op("act", lambda E: E.activation(out=col(6), in_=col(6), func=AF.Exp, scale=-1.0), reads=[bcolA], writes=[bcolA])
        ost, bost = osts[t % 2]
        for c in range(8):
            cs = slice(c * 64, (c + 1) * 64)
            n = t * 8 + c
            pk, bpk = P.bankb()
            for dc in range(2):
                P.op("pe", lambda E: E.transpose(out=pk[0:64, dc * 128:(dc + 1) * 128], in_=qk[:, 2 + dc, cs],
                                                 identity=ident[:]), reads=[bqk, bI], writes=[bpk])
            kt, bkt = kts[n % 2]
            P.op("act", lambda E: E.copy(out=kt[:], in_=pk[0:64, 0:256]), reads=[bpk], writes=[bkt])
            pS, bpS = P.bank()
            for dc in range(2):
                P.op("pe", lambda E: E.matmul(pS[0:64, 0:64], lhsT=qk[:, 2 + dc, cs], rhs=qk[:, dc, cs],
                                              start=(dc == 0), stop=(dc == 1)), reads=[bqk], writes=[bpS])
            pt, bpt = pts[n % 2]
            P.op("dve", lambda E: E.tensor_tensor(out=pt[:], in0=pS[0:64, 0:64], in1=MT, op=ALU.mult),
                 reads=[bpS, bc], writes=[bpt])
            vu, bvu = vus[n % 2]
            P.op("pool", lambda E: E.tensor_scalar(out=vu[:], in0=va[:, c, :], scalar1=colA[:, c, 5:6], scalar2=None,
                                                   op0=ALU.mult), reads=[bva, bcolA], writes=[bvu])
            P.op("dve", lambda E: E.tensor_scalar(out=C[:], in0=C[:], scalar1=WI[:, c:c + 1], scalar2=None,
                                                  op0=ALU.mult), reads=[bC, bWI], writes=[bC])
            P.op("act", lambda E: E.copy(out=Cb[:], in_=C[:]), reads=[bC], writes=[bCb])
            po, bpo = P.bank()
            P.op("pe", lambda E: E.matmul(po[0:64, 0:257], lhsT=pt[:], rhs=vu[:], start=True, stop=False),
                 reads=[bpt, bvu], writes=[bpo])
            for dc in range(2):
                P.op("pe", lambda E: E.matmul(po[0:64, 0:257], lhsT=qk[:, dc, cs], rhs=Cb[:, dc, :],
                                              start=False, stop=(dc == 1)), reads=[bqk, bCb], writes=[bpo])
            for dc in range(2):
                pu, bpu = P.bank()
                P.op("pe", lambda E: E.matmul(pu[:, 0:257], lhsT=kt[:, dc * 128:(dc + 1) * 128], rhs=vu[:],
                                              start=True, stop=True), reads=[bkt, bvu], writes=[bpu])
                P.op("dve", lambda E: E.tensor_tensor(out=C[:, dc, :], in0=C[:, dc, :], in1=pu[:, 0:257], op=ALU.add),
                     reads=[bC, bpu], writes=[bC])
            dn, bdn = P.scratch()
            P.op("act", lambda E: E.activation(out=dn[0:64, 0:1], in_=po[0:64, 256:257], func=AF.Abs),
                 reads=[bpo], writes=[bdn])
            P.op("dve", lambda E: E.tensor_scalar(out=dn[0:64, 0:1], in0=dn[0:64, 0:1], scalar1=colA[:, c, 6:7],
                                                  scalar2=None, op0=ALU.max), reads=[bdn, bcolA], writes=[bdn])
            P.op("dve", lambda E: E.reciprocal(out=dn[0:64, 0:1], in_=dn[0:64, 0:1]), reads=[bdn], writes=[bdn])
            s2, bs2 = P.scratch()
            P.op("act", lambda E: E.activation(out=junk[:], in_=po[0:64, 0:256], func=AF.Square, scale=dn[0:64, 0:1],
                                               accum_out=s2[0:64, 0:1]), reads=[bpo, bdn], writes=[bjunk, bs2])
            r, br = rstd_from(P, [(s2[0:64, 0:1], bs2)], 256)
            P.op("dve", lambda E: E.tensor_tensor(out=dn[0:64, 0:1], in0=dn[0:64, 0:1], in1=r, op=ALU.mult),
                 reads=[bdn, br], writes=[bdn])
            hm, bhm = hms[n % 2]
            P.op("dve", lambda E: E.scalar_tensor_tensor(out=hm[:], in0=po[0:64, 0:256], scalar=dn[0:64, 0:1],
                                                         in1=so[:, c, :], op0=ALU.mult, op1=ALU.mult),
                 reads=[bpo, bdn, bso], writes=[bhm])
            emit_out_T(P, hm, bhm, 64, 2, ident, bI, lambda dc: (ost[:, dc, c * 64:(c + 1) * 64], bost))
        P.dma("sp", oT.rearrange("a p n -> p a n")[:, :, t * 512:(t + 1) * 512], ost[:], reads=[bost], writes=[Buf()])
    return step


def mlstm_consts(conv, gate_bias, j):
    c = np.zeros((128, 210), np.float32)
    c[0:64, 0:64] = np.triu(np.ones((64, 64), np.float32))
    for cc in range(4):
        base = (0 if cc < 2 else 1024) + j * 256 + (cc % 2) * 128
        c[:, 64 + cc * 4:64 + cc * 4 + 4] = conv[:, base:base + 128].T
    c[:, 80] = gate_bias[0, j]
    c[:, 81] = gate_bias[1, j]
    c[:, 82:210] = 1.0
    return c


def diff_setup(nc, P, ident, bI):
    w_d = nc.dram_tensor("w_d", [2, 128, 8 * 384], F32, kind="ExternalInput").ap()
    cst_d = nc.dram_tensor("cst_d", [2, 128, 67 + 512], F32, kind="ExternalInput").ap()
    aug_d = nc.dram_tensor("aug_d", [2, 2, 512], F32, kind="ExternalInput").ap()
    lam_d = nc.dram_tensor("lam_d", [1, 256], F32, kind="ExternalInput").ap()
    sg_d = nc.dram_tensor("sg_d", [1, 128], F32, kind="ExternalInput").ap()
    lc_d = nc.dram_tensor("lc_d", [128, 2], F32, kind="ExternalInput").ap()
    oT = nc.dram_tensor("oT_d", [2, 128, S], BF16, kind="ExternalOutput").ap()
    w, bw = P.sb([128, 2, 8, 384], BF16)
    for a in range(2):
        P.dma("pool", w[:, a, :, :].rearrange("p k n -> p (k n)"), w_d[a], writes=[bw])
    cst, bc = P.sb([128, 2, 67 + 512], F32)
    for a in range(2):
        P.dma("sp", cst[:, a, :], cst_d[a], writes=[bc])
    SG, bSG = P.sb([128, 128], F32)
    P.dma("sp", SG[:], sg_d.broadcast_to([128, 128]), writes=[bSG])
    lamb, blam = P.sb([128, 256], F32)
    P.dma("sp", lamb[:], lam_d.broadcast_to([128, 256]), writes=[blam])
    lc, blc = P.sb([128, 2], F32)
    P.dma("sp", lc[:], lc_d, writes=[blc])
    pr2, bpr2 = P.sb([128, 128], F32)
    ls, bls = P.sb([128, 2], F32)
    neglam, bnl = P.sb([128, 1], F32)
    P.op("dve", lambda E: E.tensor_tensor(out=pr2[:, 0:64], in0=lamb[:, 0:64], in1=lamb[:, 64:128], op=ALU.mult),
         reads=[blam], writes=[bpr2])
    P.op("dve", lambda E: E.tensor_tensor(out=pr2[:, 64:128], in0=lamb[:, 128:192], in1=lamb[:, 192:256], op=ALU.mult),
         reads=[blam], writes=[bpr2])
    P.op("dve", lambda E: E.tensor_reduce(out=ls[:], in_=pr2[:].rearrange("p (a n) -> p a n", a=2), axis=AX.X, op=ALU.add),
         reads=[bpr2], writes=[bls])
    P.op("act", lambda E: E.activation(out=ls[:], in_=ls[:], func=AF.Exp), reads=[bls], writes=[bls])
    P.op("dve", lambda E: E.tensor_tensor(out=neglam[:], in0=ls[:, 1:2], in1=ls[:, 0:1], op=ALU.subtract),
         reads=[bls], writes=[bnl])
    P.op("dve", lambda E: E.tensor_scalar(out=neglam[:], in0=neglam[:], scalar1=lc[:, 0:1], scalar2=None, op0=ALU.add),
         reads=[bnl, blc], writes=[bnl])
    KA = [P.sb([128, S], BF16) for _ in range(2)]
    QA = [P.sb([128, 512], BF16) for _ in range(2)]
    VA, bVA = P.sb([128, S // 128, 130], BF16)
    P.op("pool", lambda E: E.memset(VA[:], 1.0), writes=[bVA])
    for c in range(2):
        P.op("pool", lambda E: E.memset(KA[c][0][64:66, :], 1.0), writes=[KA[c][1]])
    PT = [[P.sb([128, 512], BF16) for _ in range(2)] for _ in range(2)]
    tmpds = [P.sb([128, 128], F32) for _ in range(2)]
    t1, bt1 = P.sb([128, 128], F32)
    od, bod = P.sb([128, 128], F32)
    junk, bjunk = P.sb([128, 128], F32)
    hdbs = [P.sb([128, 128], BF16) for _ in range(2)]
    ostds = [P.sb([128, 512], BF16) for _ in range(2)]
    cnt = [0]

    def set_head(a):
        for c in range(2):
            P.dma("pool", QA[c][0][64:66, :], aug_d[a], writes=[QA[c][1]])

    def step(a, G, ht, bht):
        BC = cst[:, a, 0:67]
        DB = cst[:, a, 67:67 + 512].rearrange("p (i n) -> p i n", i=4)
        for c in range(2):
            ps, bps = proj_fm(P, w[:, a, :, :], bw, c * 64, 64, ht, bht)
            P.op("act", lambda E: E.activation(out=QA[c][0][0:64, :], in_=ps[0:64, :], func=AF.Copy, scale=0.125),
                 reads=[bps], writes=[QA[c][1]])
            ps, bps = proj_fm(P, w[:, a, :, :], bw, 128 + c * 64, 64, ht, bht)
            P.op("dve", lambda E: E.tensor_copy(out=KA[c][0][0:64, G * 512:(G + 1) * 512], in_=ps[0:64, :]),
                 reads=[bps], writes=[KA[c][1]])
        for s in range(4):
            ps, bps = proj_tm(P, w[:, a, :, :], bw, 256, 128, ht, bht, s * 128, 128)
            P.op("act", lambda E: E.copy(out=VA[:, G * 4 + s, 0:128], in_=ps[:, 0:128]), reads=[bps], writes=[bVA])
        Sb = [P.pf[4], P.pf[5]]
        ostd, bostd = ostds[G % 2]
        for hg in range(2):
            g = 2 * G + hg
            q0 = hg * 256
            nkb = 2 * g + 2

            def scores(kb):
                i0 = max(0, kb - 2 * g)
                for c in range(2):
                    S_, bS = Sb[c]
                    P.op("pe", lambda E: E.matmul(S_[:, i0 * 128:256], lhsT=KA[c][0][0:66, kb * 128:(kb + 1) * 128],
                                                  rhs=QA[c][0][0:66, q0 + i0 * 128:q0 + 256], start=True, stop=True),
                         reads=[KA[c][1], QA[c][1]], writes=[bS])

            scores(0)
            for kb in range(nkb):
                i0 = max(0, kb - 2 * g)
                pts_ = []
                for c in range(2):
                    S_, bS = Sb[c]
                    pt, bpt = PT[c][kb % 2]
                    pts_.append((pt, bpt))
                    if kb < 2 * g:
                        m = 2 * g - kb
                        P.op("act", lambda E: E.activation(out=pt[:, 0:256], in_=S_[:, 0:256], func=AF.Exp,
                                                           bias=BC[:, m + 3:m + 4]), reads=[bS, bc], writes=[bpt])
                    else:
                        td, btd = tmpds[c]
                        P.op("dve", lambda E: E.tensor_tensor(out=td[:], in0=S_[:, i0 * 128:(i0 + 1) * 128],
                                                              in1=DB[:, i0, :], op=ALU.add), reads=[bS, bc], writes=[btd])
                        P.op("act", lambda E: E.activation(out=pt[:, i0 * 128:(i0 + 1) * 128], in_=td[:], func=AF.Exp),
                             reads=[btd], writes=[bpt])
                        if i0 < 1:
                            P.op("act", lambda E: E.activation(out=pt[:, 128:256], in_=S_[:, 128:256],
                                                               func=AF.Exp, bias=BC[:, 3:4]),
                                 reads=[bS, bc], writes=[bpt])
                if kb + 1 < nkb:
                    scores(kb + 1)
                for i in range(i0, 2):
                    for c in range(2):
                        pt, bpt = pts_[c]
                        acc_, bacc_ = P.pf[2 * i + c]
                        P.op("pe", lambda E: E.matmul(acc_[:, 0:129], lhsT=pt[:, i * 128:(i + 1) * 128],
                                                      rhs=VA[:, kb, 0:129], start=(kb == 0), stop=(kb == 2 * g + i)),
                             reads=[bpt, bVA], writes=[bacc_])
            for i in range(2):
                a0, ba0 = P.pf[2 * i]
                a1, ba1 = P.pf[2 * i + 1]
                r1, br1 = P.scratch()
                r2, br2 = P.scratch()
                P.op("dve", lambda E: E.reciprocal(out=r1[:, 0:1], in_=a0[:, 128:129]), reads=[ba0], writes=[br1])
                P.op("dve", lambda E: E.reciprocal(out=r2[:, 0:1], in_=a1[:, 128:129]), reads=[ba1], writes=[br2])
                P.op("dve", lambda E: E.tensor_tensor(out=r2[:, 0:1], in0=r2[:, 0:1], in1=neglam[:], op=ALU.mult),
                     reads=[br2, bnl], writes=[br2])
                P.op("act", lambda E: E.activation(out=t1[:], in_=a0[:, 0:128], func=AF.Copy, scale=r1[:, 0:1]),
                     reads=[ba0, br1], writes=[bt1])
                P.op("dve", lambda E: E.scalar_tensor_tensor(out=od[:], in0=a1[:, 0:128], scalar=r2[:, 0:1], in1=t1[:],
                                                             op0=ALU.mult, op1=ALU.add), reads=[ba1, br2, bt1], writes=[bod])
                ss = sumsq(P, od[:], bod, junk[:], bjunk)
                r, br = rstd_from(P, [ss], 128)
                P.op("dve", lambda E: E.tensor_scalar(out=r, in0=r, scalar1=lc[:, 1:2], scalar2=None, op0=ALU.mult),
                     reads=[br, blc], writes=[br])
                hdb, bhdb = hdbs[cnt[0] % 2]
                cnt[0] += 1
                P.op("dve", lambda E: E.scalar_tensor_tensor(out=hdb[:], in0=od[:], scalar=r, in1=SG[:], op0=ALU.mult,
                                                             op1=ALU.mult), reads=[bod, br, bSG], writes=[bhdb])
                emit_out_T(P, hdb, bhdb, 128, 1, ident, bI,
                           lambda dc: (ostd[:, q0 + i * 128:q0 + (i + 1) * 128], bostd))
        P.dma("sp", oT[a][:, G * 512:(G + 1) * 512], ostd[:], reads=[bostd], writes=[Buf()])
    return set_head, step


def diff_consts(slope):
    rk = np.arange(128, dtype=np.float64)
    c = np.zeros((128, 67 + 512), np.float64)
    for mi in range(67):
        c[:, mi] = slope * (rk - 128.0 * (mi - 3))
    rq = np.arange(128, dtype=np.float64)
    allowed = (rk[:, None] // 64) <= (rq[None, :] // 64)
    for i0 in range(4):
        db = -slope * np.abs(rq[None, :] - rk[:, None]) + slope * (128.0 * i0 + rq[None, :])
        c[:, 67 + i0 * 128:67 + (i0 + 1) * 128] = np.where(allowed, db, -30000.0)
    col = np.arange(512) % 256
    aug = np.stack([-slope * 128.0 * (col // 128), -slope * (col % 128)]).astype(np.float32)
    return c.astype(np.float32), aug


def build_k2(parts=("r", "m", "d"), ntiles=S // 512):
    nc = bass.Bass("TRN2", target_bir_lowering=False)
    with ExitStack() as ctx:
        P = Prog(nc, ctx)
        hT = nc.dram_tensor("hT", [D, S], BF16, kind="ExternalInput").ap()
        ident_d = nc.dram_tensor("ident", [128, 128], BF16, kind="ExternalInput").ap()
        ident, bI = P.sb([128, 128], BF16)
        P.dma("sp", ident[:], ident_d, writes=[bI])
        hts = [P.sb([128, 8, 512], BF16) for _ in range(2)]
        hTv = hT.rearrange("(k p) n -> p k n", p=128)
        nload = [0]

        def load_h(t):
            ht, bht = hts[nload[0] % 2]
            nload[0] += 1
            P.dma("sp", ht[:], hTv[:, :, t * 512:(t + 1) * 512], writes=[bht])
            return ht, bht
        rstep = ret_setup(nc, P, ident, bI) if "r" in parts else None
        mstep = mlstm_setup(nc, P, ident, bI) if "m" in parts else None
        if "d" in parts:
            set_head, dstep = diff_setup(nc, P, ident, bI)
            set_head(0)
        for t in range(ntiles):
            ht, bht = load_h(t)
            if rstep:
                rstep(t, ht, bht)
            if mstep:
                mstep(t, ht, bht)
            if "d" in parts:
                dstep(0, t, ht, bht)
        if "d" in parts:
            set_head(1)
            for t in range(ntiles):
                ht, bht = load_h(t)
                dstep(1, t, ht, bht)
        P.finish()
        P.emit()
    return nc


def k2_inputs(l, j, w_in, conv, gate_bias, diff_lambda, diff_subln):
    def cols(names, width, idx):
        return np.concatenate([np.arange(OFF[n] + idx * width, OFF[n] + (idx + 1) * width) for n in names])
    gamma = 1.0 - 2.0 ** (-5.0 - j)
    wr = kp(w_in[:, cols(("rq", "rk", "rv", "rg"), 256, j)])
    wm = kp(w_in[:, np.concatenate([cols(("mq", "mk", "mv", "mo"), 256, j), [OFF["mi"] + j, OFF["mf"] + j]])])
    wd, cd, ad = [], [], []
    for a in range(2):
        hidx = 2 * j + a
        cq = OFF["dq"] + hidx * 128 + np.arange(128)
        ck = OFF["dk"] + hidx * 128 + np.arange(128)
        cv = OFF["dv"] + hidx * 128 + np.arange(128)
        wd.append(kp(w_in[:, np.concatenate([cq, ck, cv])]).reshape(128, -1))
        c_, a_ = diff_consts(2.0 ** (-(hidx + 1.0)))
        cd.append(c_)
        ad.append(a_)
    return dict(w_r=np.ascontiguousarray(wr.reshape(128, -1)), cst_r=ret_consts(gamma),
                w_m=np.ascontiguousarray(wm.reshape(128, -1)), cst_m=mlstm_consts(conv, gate_bias, j),
                w_d=np.ascontiguousarray(np.stack(wd)), cst_d=np.stack(cd), aug_d=np.stack(ad),
                lam_d=np.ascontiguousarray(diff_lambda.reshape(1, 256)), sg_d=np.ascontiguousarray(diff_subln.reshape(1, 128)),
                lc_d=np.tile(np.array([[-lam_init_of(l), 1.0 - lam_init_of(l)]], np.float32), (128, 1)),
                ident=IDENT)


def lam_init_of(l):
    return 0.8 - 0.6 * math.exp(-0.3 * l)


_PROGS = {}


def _prog(name, builder):
    if name not in _PROGS:
        _PROGS[name] = builder()
    return _PROGS[name]


def _run(nc, in_maps):
    res = run_bass_kernel_spmd(nc, in_maps, core_ids=list(range(NCORE)))
    return res.results


def kernel(x, norm_gains, w_in, mlstm_conv, mlstm_gate_bias, diff_lambda, diff_subln,
           w_mlstm_out, w_diff_out, w_ret_out, w_out, w_ffn_in, w_ffn_out):
    f = lambda a: np.ascontiguousarray(np.asarray(a, dtype=np.float32))
    x, norm_gains, w_in = f(x), f(norm_gains), f(w_in)
    xs = [np.ascontiguousarray(x.reshape(B * S, D)[c * TOK:(c + 1) * TOK]) for c in range(NCORE)]
    r = _run(build_k1(), [dict(x=xs[c], g=norm_gains[0, 0:1], ident=IDENT) for c in range(NCORE)])
    hT = [r[c]["hT"] for c in range(NCORE)]
    for l in range(DEPTH):
        hfull = [np.ascontiguousarray(np.concatenate(hT[b * 4:(b + 1) * 4], axis=1)) for b in range(B)]
        ims = []
        for c in range(NCORE):
            b, j = divmod(c, 4)
            im = k2_inputs(l, j, w_in[l], f(mlstm_conv[l]), f(mlstm_gate_bias[l]), f(diff_lambda[l]), f(diff_subln[l]))
            im["hT"] = hfull[b]
            ims.append(im)
        r = _run(build_k2(), ims)
        bT = []
        for c in range(NCORE):
            b, q = divmod(c, 4)
            per = []
            for key in ("oT_m", "oT_d", "oT_r"):
                full = np.concatenate([r[b * 4 + j][key].reshape(256, S) for j in range(4)], axis=0)
                per.append(full[:, q * TOK:(q + 1) * TOK])
            bT.append(np.ascontiguousarray(np.stack(per)))
        wk = prep_k3_weights(w_in[l], f(w_mlstm_out[l]), f(w_diff_out[l]), f(w_ret_out[l]), f(w_out[l]),
                             f(w_ffn_in[l]), f(w_ffn_out[l]))
        gnext = norm_gains[l + 1, 0:1] if l + 1 < DEPTH else np.ones((1, D), np.float32)
        g4 = np.ascontiguousarray(np.concatenate([norm_gains[l, 1:4], gnext], axis=0))
        r = _run(build_k3(), [dict(x=xs[c], hT=hT[c], bT=bT[c], g=g4, ident=IDENT, **wk) for c in range(NCORE)])
        xs = [r[c]["xo"] for c in range(NCORE)]
        hT = [r[c]["hTn"] for c in range(NCORE)]
    return np.concatenate(xs, axis=0).reshape(B, S, D).astype(np.float32)
```
